# Optimizing a Trainium2 kernel written in Bass

```python
import jax, jax.numpy as jnp
from jax import lax
import numpy as np

D_MODEL = 1024
BATCH = 16
SEQ = 2048
DEPTH = 1

GRID_W = 64
CTX_LEN = 256
ATT_HEADS = 8
ATT_KV_HEADS = 2
ATT_GROUP = ATT_HEADS // ATT_KV_HEADS
ATT_HEAD_DIM = 64
ATT_WIDTH = ATT_HEADS * ATT_HEAD_DIM
ATT_KV_WIDTH = ATT_KV_HEADS * ATT_HEAD_DIM
WINDOW = 128
BLOCK = 128
ROPE_BASE = 10000.0
HG_WIDTH = D_MODEL // 2
HG_EXPAND = 128
HG_HEADS = HG_WIDTH // HG_EXPAND
CHUNK = 64
D_FF = 2816
N_MOD = 9
IN_WIDTH = ATT_WIDTH + 2 * ATT_KV_WIDTH + 5 * HG_WIDTH + 2 * D_MODEL
EPS = 1e-6
NEG_INF = -1e30

kernel_name = 'hybrid_dit_gqa_hgrn2_macaron'


def rmsnorm(x, g):
    x32 = x.astype(jnp.float32)
    y = x32 * lax.rsqrt(jnp.mean(x32 * x32, axis=-1, keepdims=True) + EPS)
    return (y * g.astype(jnp.float32)).astype(x.dtype)


def swiglu(h, w_in, w_out):
    gate, up = jnp.split(h @ w_in, 2, axis=-1)
    return (jax.nn.silu(gate) * up) @ w_out


def ada_pre(z, g, shift, scale):
    return rmsnorm(z, g) * (1 + scale) + shift


def ada_post(z, y, g, gate, w):
    return z + w * gate * rmsnorm(y, g)


def rope_1d(x, pos):
    half = x.shape[-1] // 2
    freqs = ROPE_BASE ** (-jnp.arange(half, dtype=jnp.float32) / half)
    ang = pos.astype(jnp.float32)[:, None] * freqs[None, :]
    cos, sin = jnp.cos(ang).astype(x.dtype), jnp.sin(ang).astype(x.dtype)
    x1, x2 = x[..., :half], x[..., half:]
    return jnp.concatenate([x1 * cos - x2 * sin, x1 * sin + x2 * cos], axis=-1)


def rope_2d(x, row, col):
    h = x.shape[-1] // 2
    return jnp.concatenate([rope_1d(x[..., :h], row), rope_1d(x[..., h:], col)], axis=-1)


def split_in(p):
    sizes = (ATT_WIDTH, ATT_KV_WIDTH, ATT_KV_WIDTH, HG_WIDTH, HG_WIDTH, HG_WIDTH, HG_WIDTH, HG_WIDTH, D_MODEL, D_MODEL)
    points = [int(v) for v in np.cumsum(sizes)[:-1]]
    return jnp.split(p, points, axis=-1)


def q_heads(z):
    B, T, _ = z.shape
    return z.reshape(B, T, ATT_KV_HEADS, ATT_GROUP, ATT_HEAD_DIM).transpose(0, 2, 3, 1, 4)


def kv_heads(z):
    B, T, _ = z.shape
    return z.reshape(B, T, ATT_KV_HEADS, ATT_HEAD_DIM).transpose(0, 2, 1, 3)


def hg_heads(z):
    B, T, _ = z.shape
    return z.reshape(B, T, HG_HEADS, HG_EXPAND).transpose(0, 2, 1, 3)


def hg_merge(o):
    B, H, T, d = o.shape
    return o.transpose(0, 2, 1, 3).reshape(B, T, H * d)


def softmax_with_sink(logits, sink):
    sink_col = jnp.broadcast_to(sink, logits.shape[:-1] + (1,))
    p = jax.nn.softmax(jnp.concatenate([logits, sink_col], axis=-1), axis=-1)
    return p[..., :-1]


def window_attention(q, k, v, kc, vc, sink):
    B, KVH, G, T, hd = q.shape
    nb = T // BLOCK
    scale = hd ** -0.5
    qb = q.reshape(B, KVH, G, nb, BLOCK, hd)

    def band(z):
        zp = jnp.pad(z, ((0, 0), (0, 0), (BLOCK, BLOCK), (0, 0))).reshape(B, KVH, nb + 2, BLOCK, hd)
        return jnp.concatenate([zp[:, :, :nb], zp[:, :, 1:nb + 1], zp[:, :, 2:]], axis=3)

    kb, vb = band(k), band(v)
    blk = jnp.arange(nb)[:, None, None] * BLOCK
    qpos = blk + jnp.arange(BLOCK)[None, :, None]
    kpos = blk - BLOCK + jnp.arange(3 * BLOCK)[None, None, :]
    valid = (jnp.abs(kpos - qpos) <= WINDOW) & (kpos >= 0) & (kpos < T)
    s_band = jnp.einsum('bkgnqd,bknsd->bkgnqs', qb, kb).astype(jnp.float32) * scale
    s_band = jnp.where(valid, s_band, NEG_INF)
    s_ctx = jnp.einsum('bkgnqd,bksd->bkgnqs', qb, kc).astype(jnp.float32) * scale
    sink_b = sink.astype(jnp.float32).reshape(KVH, G)[None, :, :, None, None, None]
    p = softmax_with_sink(jnp.concatenate([s_band, s_ctx], axis=-1), sink_b).astype(v.dtype)
    o = (jnp.einsum('bkgnqs,bknsd->bkgnqd', p[..., :3 * BLOCK], vb)
         + jnp.einsum('bkgnqs,bksd->bkgnqd', p[..., 3 * BLOCK:], vc))
    return o.reshape(B, KVH, G, T, hd).transpose(0, 3, 1, 2, 4).reshape(B, T, KVH * G * hd)


def context_attention(qc, kc, vc, sink):
    B, KVH, G, L, hd = qc.shape
    s = jnp.einsum('bkgqd,bksd->bkgqs', qc, kc).astype(jnp.float32) * (hd ** -0.5)
    sink_b = sink.astype(jnp.float32).reshape(KVH, G)[None, :, :, None, None]
    p = softmax_with_sink(s, sink_b).astype(vc.dtype)
    o = jnp.einsum('bkgqs,bksd->bkgqd', p, vc)
    return o.transpose(0, 3, 1, 2, 4).reshape(B, L, KVH * G * hd)


def gated_scan(q, k, v, logf, s0):
    B, H, T, dk = q.shape
    dv = v.shape[-1]
    n = T // CHUNK

    def chunks(z):
        return jnp.moveaxis(z.reshape(B, H, n, CHUNK, z.shape[-1]), 2, 0)

    tril = jnp.tril(jnp.ones((CHUNK, CHUNK), dtype=bool))

    def step(S, inp):
        qc, kc, vc, lf = inp
        b = jnp.cumsum(lf, axis=-2)
        o_inter = jnp.einsum('bhtd,bhde->bhte', qc * jnp.exp(b), S)
        diff = b[:, :, :, None, :] - b[:, :, None, :, :]
        decay = jnp.where(tril[:, :, None], jnp.exp(jnp.minimum(diff, 0.0)), 0.0)
        scores = jnp.einsum('bhtsd,bhsd->bhts', qc[:, :, :, None, :] * decay, kc)
        o_intra = jnp.einsum('bhts,bhse->bhte', scores, vc)
        b_last = b[:, :, -1:, :]
        k_dec = kc * jnp.exp(b_last - b)
        S_new = jnp.exp(b_last[:, :, 0, :])[..., None] * S + jnp.einsum('bhsd,bhse->bhde', k_dec, vc)
        return S_new, o_inter + o_intra

    S_fin, o = lax.scan(step, s0, (chunks(q), chunks(k), chunks(v), chunks(logf)))
    return jnp.moveaxis(o, 0, 2).reshape(B, H, T, dv), S_fin


def hgrn_bidir(q, v, lf_f, k_f, lf_b, k_b, s0_f, s0_b):
    flip = lambda z: jnp.flip(z, axis=2)
    o_f, s_f = gated_scan(q, k_f, v, lf_f, s0_f)
    o_b, s_b = gated_scan(flip(q), flip(k_b), flip(v), flip(lf_b), s0_b)
    return o_f + flip(o_b), s_f, s_b


def hg_forget(z, lb):
    f = lb + (1.0 - lb) * jax.nn.sigmoid(z.astype(jnp.float32))
    return hg_heads(jnp.log(f)), hg_heads(1.0 - f)


def mixer(h, hc, w_in, sink, lb_f, lb_b, g_hnorm, w_o_attn, w_o_hgrn, w_out, need_ctx):
    B, T, _ = h.shape
    rows = T // GRID_W
    row = jnp.repeat(jnp.arange(rows, dtype=jnp.int32), GRID_W)
    col = jnp.tile(jnp.arange(GRID_W, dtype=jnp.int32), rows)
    f32 = jnp.float32
    aq, ak, av, hq, hff, hfb, hi, hg, ga, gh = split_in(h @ w_in)
    aqc, akc, avc, hqc, hffc, hfbc, hic, hgc, gac, ghc = split_in(hc @ w_in)
    q = rope_2d(q_heads(aq), row, col)
    k = rope_2d(kv_heads(ak), row, col)
    kc, vc = kv_heads(akc), kv_heads(avc)
    o_att = window_attention(q, k, kv_heads(av), kc, vc, sink)
    qhc = hg_heads(jax.nn.silu(hqc.astype(f32)))
    vhc = hg_heads(hic.astype(f32))
    lfc_f, kc_f = hg_forget(hffc, lb_f)
    lfc_b, kc_b = hg_forget(hfbc, lb_b)
    s0 = jnp.zeros((B, HG_HEADS, HG_EXPAND, HG_EXPAND), f32)
    o_hc, s_f, s_b = hgrn_bidir(qhc, vhc, lfc_f, kc_f, lfc_b, kc_b, s0, s0)
    qh = hg_heads(jax.nn.silu(hq.astype(f32)))
    vh = hg_heads(hi.astype(f32))
    lf_f, k_f = hg_forget(hff, lb_f)
    lf_b, k_b = hg_forget(hfb, lb_b)
    o_h, _, _ = hgrn_bidir(qh, vh, lf_f, k_f, lf_b, k_b, s_f, s_b)

    def readout(o, gate):
        return rmsnorm(hg_merge(o), g_hnorm).astype(gate.dtype) * jax.nn.silu(gate)

    def merge(o_a, o_r, g_a, g_r):
        return (jax.nn.sigmoid(g_a) * (o_a @ w_o_attn) + jax.nn.sigmoid(g_r) * (o_r @ w_o_hgrn)) @ w_out

    y = merge(o_att, readout(o_h, hg), ga, gh)
    if not need_ctx:
        return y, None
    o_attc = context_attention(q_heads(aqc), kc, vc, sink)
    yc = merge(o_attc, readout(o_hc, hgc), gac, ghc)
    return y, yc


def setup_inputs(seed: int = 0) -> dict:
    key = jax.random.key(seed)
    ks = jax.random.split(key, 20)
    f32 = jnp.float32

    def nrm(k, shape, scale):
        return jax.random.normal(k, shape, f32) * scale

    D = D_MODEL
    return {
        'x': nrm(ks[0], (BATCH, SEQ, D), 1.0),
        'c': nrm(ks[1], (BATCH, D), 1.0),
        'ctx': nrm(ks[2], (BATCH, CTX_LEN, D), 1.0),
        'c_ctx': nrm(ks[3], (D,), 1.0),
        'w_ada': nrm(ks[4], (DEPTH, D, N_MOD * D), 0.5 * D ** -0.5),
        'b_ada': nrm(ks[5], (DEPTH, N_MOD * D), 0.02),
        'norm_pre': 1.0 + nrm(ks[6], (DEPTH, 3, D), 0.02),
        'norm_post': 1.0 + nrm(ks[7], (DEPTH, 3, D), 0.02),
        'ffn1_w_in': nrm(ks[8], (DEPTH, D, 2 * D_FF), D ** -0.5),
        'ffn1_w_out': nrm(ks[9], (DEPTH, D_FF, D), D_FF ** -0.5),
        'ffn2_w_in': nrm(ks[10], (DEPTH, D, 2 * D_FF), D ** -0.5),
        'ffn2_w_out': nrm(ks[11], (DEPTH, D_FF, D), D_FF ** -0.5),
        'mix_w_in': nrm(ks[12], (DEPTH, D, IN_WIDTH), D ** -0.5),
        'attn_sink': nrm(ks[13], (DEPTH, ATT_HEADS), 0.5),
        'hgrn_lb_fwd': nrm(ks[14], (DEPTH + 1, HG_WIDTH), 0.5),
        'hgrn_lb_bwd': nrm(ks[15], (DEPTH + 1, HG_WIDTH), 0.5),
        'hgrn_norm': 1.0 + nrm(ks[16], (DEPTH, HG_WIDTH), 0.02),
        'w_o_attn': nrm(ks[17], (DEPTH, ATT_WIDTH, D), ATT_WIDTH ** -0.5),
        'w_o_hgrn': nrm(ks[18], (DEPTH, HG_WIDTH, D), HG_WIDTH ** -0.5),
        'w_out': nrm(ks[19], (DEPTH, D, D), D ** -0.5),
    }


def reference(x, c, ctx, c_ctx, w_ada, b_ada, norm_pre, norm_post, ffn1_w_in, ffn1_w_out,
              ffn2_w_in, ffn2_w_out, mix_w_in, attn_sink, hgrn_lb_fwd, hgrn_lb_bwd, hgrn_norm,
              w_o_attn, w_o_hgrn, w_out):
    B, T, D = x.shape
    lb_f_all = jnp.cumsum(jax.nn.softmax(hgrn_lb_fwd.astype(jnp.float32), axis=0), axis=0)
    lb_b_all = jnp.cumsum(jax.nn.softmax(hgrn_lb_bwd.astype(jnp.float32), axis=0), axis=0)
    xc = ctx
    for l in range(DEPTH):
        need_ctx = l < DEPTH - 1
        mod = (jax.nn.silu(c) @ w_ada[l] + b_ada[l]).reshape(B, N_MOD, 1, D)
        mod_c = (jax.nn.silu(c_ctx) @ w_ada[l] + b_ada[l]).reshape(N_MOD, D)
        m = [mod[:, j] for j in range(N_MOD)]
        mc = [mod_c[j] for j in range(N_MOD)]
        h = ada_pre(x, norm_pre[l, 0], m[0], m[1])
        x = ada_post(x, swiglu(h, ffn1_w_in[l], ffn1_w_out[l]), norm_post[l, 0], m[2], 0.5)
        hc = ada_pre(xc, norm_pre[l, 0], mc[0], mc[1])
        xc = ada_post(xc, swiglu(hc, ffn1_w_in[l], ffn1_w_out[l]), norm_post[l, 0], mc[2], 0.5)
        h = ada_pre(x, norm_pre[l, 1], m[3], m[4])
        hc = ada_pre(xc, norm_pre[l, 1], mc[3], mc[4])
        y, yc = mixer(h, hc, mix_w_in[l], attn_sink[l], lb_f_all[l], lb_b_all[l], hgrn_norm[l],
                      w_o_attn[l], w_o_hgrn[l], w_out[l], need_ctx)
        x = ada_post(x, y, norm_post[l, 1], m[5], 1.0)
        h = ada_pre(x, norm_pre[l, 2], m[6], m[7])
        x = ada_post(x, swiglu(h, ffn2_w_in[l], ffn2_w_out[l]), norm_post[l, 2], m[8], 0.5)
        if need_ctx:
            xc = ada_post(xc, yc, norm_post[l, 1], mc[5], 1.0)
            hc = ada_pre(xc, norm_pre[l, 2], mc[6], mc[7])
            xc = ada_post(xc, swiglu(hc, ffn2_w_in[l], ffn2_w_out[l]), norm_post[l, 2], mc[8], 0.5)
    return x
```

```python
import contextlib
import numpy as np
import ml_dtypes
import concourse.bass as bass
import concourse.mybir as mybir
from concourse.bass_utils import run_bass_kernel_spmd

F32 = mybir.dt.float32
BF16 = mybir.dt.bfloat16
AF = mybir.ActivationFunctionType
ALU = mybir.AluOpType
AX = mybir.AxisListType

D = 1024
DFF = 2816
T = 2048
L = 256
NTOK = T + L
EPS = 1e-6
NEG = -1e30
NCORES = 8


class Tok:
    __slots__ = ("sem", "val", "eid")

    def __init__(self, sem, val, eid):
        self.sem, self.val, self.eid = sem, val, eid


class Buf:
    def __init__(self, ap):
        self.ap = ap
        self.w = None
        self.r = {}
        self.dsem = None

    def __getitem__(self, k):
        return self.ap[k]


class Eng:
    def __init__(self, raw, eid, sem, is_pe=False):
        self.raw, self.eid, self.sem, self.is_pe = raw, eid, sem, is_pe
        self.cnt = 0
        self.waited = {}

    def wait(self, t):
        if t is None:
            return
        if self.is_pe and t.eid == self.eid:
            return
        k = t.sem.num
        if self.waited.get(k, 0) >= t.val:
            return
        self.raw.wait_ge(t.sem, t.val)
        self.waited[k] = t.val


class KB:
    def __init__(self, nc, es):
        self.nc, self.es = nc, es
        mk = lambda n: es.enter_context(nc.semaphore(n))
        self.pe = Eng(nc.tensor, 0, mk("s_pe"), True)
        self.act = Eng(nc.scalar, 1, mk("s_act"))
        self.dve = Eng(nc.vector, 2, mk("s_dve"))
        self.pool = Eng(nc.gpsimd, 3, mk("s_pool"))
        self.sp = Eng(nc.sync, 4, mk("s_sp"))
        self.engs = [self.pe, self.act, self.dve, self.pool, self.sp]
        self.free_dsems = []
        self.nds = 0
        self.semcnt = {}
        self.outstanding = {}
        self.phase_dsems = []

    def get_dsem(self):
        if self.free_dsems:
            s = self.free_dsems.pop()
        else:
            s = self.es.enter_context(self.nc.semaphore("ds%d" % self.nds))
            self.nds += 1
            self.semcnt[s.num] = 0
        self.phase_dsems.append(s)
        return s

    def op(self, eng, fn, reads=(), writes=()):
        for b in reads:
            eng.wait(b.w)
        for b in writes:
            eng.wait(b.w)
            for t in b.r.values():
                eng.wait(t)
        inst = fn(eng.raw)
        eng.cnt += 1
        inst.then_inc(eng.sem, 1)
        t = Tok(eng.sem, eng.cnt, eng.eid)
        for b in reads:
            b.r[eng.eid] = t
        for b in writes:
            b.w = t
            b.r = {}
        return t

    def dma(self, out_ap, in_ap, reads=(), writes=(), q=None):
        q = q or self.sp
        for b in reads:
            q.wait(b.w)
        for b in writes:
            q.wait(b.w)
            for t in b.r.values():
                q.wait(t)
        owner = (list(writes) + list(reads))[0]
        if owner.dsem is None:
            owner.dsem = self.get_dsem()
        inst = q.raw.dma_start(out=out_ap, in_=in_ap)
        n = self.semcnt[owner.dsem.num] + 16
        self.semcnt[owner.dsem.num] = n
        inst.then_inc(owner.dsem, 16)
        t = Tok(owner.dsem, n, 100 + owner.dsem.num)
        for b in reads:
            b.r[t.eid] = t
        for b in writes:
            b.w = t
            b.r = {}
        self.outstanding[owner.dsem.num] = t
        return t

    def barrier(self):
        toks = [Tok(e.sem, e.cnt, e.eid) for e in self.engs if e.cnt > 0]
        toks += list(self.outstanding.values())
        for e in self.engs:
            for t in toks:
                if e.is_pe and t.eid == e.eid:
                    continue
                e.wait(t)
        self.outstanding = {}
        self.free_dsems += self.phase_dsems
        self.phase_dsems = []


class Ring:
    def __init__(self, bufs):
        self.bufs, self.i = bufs, 0

    def next(self):
        b = self.bufs[self.i % len(self.bufs)]
        self.i += 1
        return b


class _Stop(Exception):
    pass


def build_program(dbg=False, lvl=9):
    nc = bass.Bass("TRN2", target_bir_lowering=False)
    es = contextlib.ExitStack()

    def din(name, shape, dt=F32):
        return nc.dram_tensor(name, list(shape), dt, kind="ExternalInput").ap()

    def dscr(name, shape, dt=F32):
        return nc.dram_tensor(name, list(shape), dt, kind="ExternalOutput" if dbg else "Internal").ap()

    x_d = din("x", [2, T, D])
    ctx_d = din("ctx", [2, L, D])
    cT_d = din("cT", [128, 24])
    wada_d = din("w_ada", [D, 9 * D])
    bada_d = din("b_ada", [1, 9 * D])
    badaT_d = din("b_adaT", [128, 72])
    gpreT_d = din("gpreT", [128, 24])
    gpost_d = din("norm_post", [3, D])
    f1i_d = din("ffn1_w_in", [D, 2 * DFF])
    f1o_d = din("ffn1_w_out", [DFF, D])
    f2i_d = din("ffn2_w_in", [D, 2 * DFF])
    f2o_d = din("ffn2_w_out", [DFF, D])
    wmix_d = din("mix_w_in", [D, 5376])
    sink_d = din("attn_sink", [1, 8])
    lbf_d = din("hgrn_lb_fwd", [2, 512])
    lbb_d = din("hgrn_lb_bwd", [2, 512])
    ghn_d = din("hgrn_norm", [1, 512])
    woa_d = din("w_o_attn", [512, D])
    woh_d = din("w_o_hgrn", [512, D])
    wout_d = din("w_out", [D, D])
    ident_d = din("c_ident", [128, 128], BF16)
    amask_d = din("c_amask", [128, 3 * 128])
    hmask_d = din("c_hmask", [128, 128])
    lmat_d = din("c_lmat", [128, 2 * 128])
    cind_d = din("c_cind", [128, 2])
    cos_d = din("c_cos", [T, 640])
    sin_d = din("c_sin", [T, 640])
    out_d = nc.dram_tensor("out", [2, T, D], F32, kind="ExternalOutput").ap()

    x1_d = dscr("x1", [2, NTOK, D])
    x2_d = dscr("x2", [2, T, D])
    gbc_d = dscr("gbc", [9, D])
    QT_d = dscr("QT", [2, 4, 128, T], BF16)
    KT_d = dscr("KT", [2, 128, NTOK], BF16)
    VA_d = dscr("VA", [2, NTOK, 128], BF16)
    QHT_d = dscr("QHT", [2, 8, 128, T], BF16)
    KHT_d = dscr("KHT", [2, 8, 128, T], BF16)
    KH_d = dscr("KH", [2, NTOK, 1024], BF16)
    VH_d = dscr("VH", [2, NTOK, 512], BF16)
    G2_d = dscr("G2", [2, T, 512])
    SG_d = dscr("SG", [2, 16, 128, T])
    if dbg:
        OAT_d = dscr("OAT", [2, 128, 4, T], BF16)
        SDEC_d = dscr("SDEC", [2, 128, 8, 32, 128], BF16)

    kb = KB(nc, es)
    pe, act, dve, pool = kb.pe, kb.act, kb.dve, kb.pool

    sbn = [0]

    def sb(name, shape, dt=F32, stack=None):
        sbn[0] += 1
        return (stack or es).enter_context(nc.sbuf_tensor("s%d_%s" % (sbn[0], name), list(shape), dt))

    pp = [es.enter_context(nc.psum_tensor("pp%d" % i, [128, 1024], F32)) for i in range(4)]
    PB = [Buf(pp[k // 2][:, (k % 2) * 512:(k % 2 + 1) * 512]) for k in range(8)]

    def bank_bf(k):
        return PB[k].ap.bitcast(BF16)

    ident = Buf(sb("ident", [128, 128], BF16))
    neghalf = Buf(sb("neghalf", [128, 1]))
    preA = Buf(sb("preA", [128, 3, 3, 8]))
    preB = Buf(sb("preB", [128, 3, 3, 8]))
    elast = Buf(sb("elast", [128, 2, 2, 4, 36]))
    statt = sb("stats", [128, 48])
    stat_ring = Ring([Buf(statt[:, i:i + 1]) for i in range(48)])
    junk = sb("junk", [128, 1024], BF16)

    kb.dma(ident.ap[:], ident_d[:, :], writes=[ident])
    kb.op(dve, lambda e: e.memset(neghalf.ap[:], -0.5), writes=[neghalf])

    def rstd_from_ss(ss, n):
        v = stat_ring.next()
        kb.op(dve, lambda e: e.tensor_scalar(out=v.ap, in0=ss.ap, scalar1=1.0 / n, scalar2=EPS,
                                             op0=ALU.mult, op1=ALU.add), reads=[ss], writes=[v])
        r = stat_ring.next()
        kb.op(pool, lambda e: e.tensor_tensor(out=r.ap, in0=v.ap, in1=neghalf.ap[:], op=ALU.pow),
              reads=[v, neghalf], writes=[r])
        return r

    cast_rr = [0]

    def cast(out_ap, in_ap, reads, writes):
        i = cast_rr[0] % 3
        cast_rr[0] += 1
        if i == 0:
            kb.op(pool, lambda e: e.tensor_copy(out=out_ap, in_=in_ap), reads=reads, writes=writes)
        elif i == 1:
            kb.op(dve, lambda e: e.tensor_copy(out=out_ap, in_=in_ap), reads=reads, writes=writes)
        else:
            kb.op(act, lambda e: e.copy(out=out_ap, in_=in_ap), reads=reads, writes=writes)

    def load_w(dst_ap, src_ap, n, stage_ring, wbuf):
        c0 = 0
        while c0 < n:
            m = min(1024, n - c0)
            st = stage_ring.next()
            kb.dma(st.ap[:, 0:m], src_ap[:, c0:c0 + m], writes=[st])
            cast(dst_ap[:, c0:c0 + m], st.ap[:, 0:m], [st], [wbuf])
            c0 += m

    def pre_a(src_ap, xin_ring, xn_ring):
        xin = xin_ring.next()
        kb.dma(xin.ap[:], src_ap, writes=[xin])
        ss = stat_ring.next()
        kb.op(act, lambda e: e.activation(out=junk[:, :], in_=xin.ap[:], func=AF.Square, accum_out=ss.ap),
              reads=[xin], writes=[ss])
        r = rstd_from_ss(ss, D)
        xn = xn_ring.next()
        kb.op(pool, lambda e: e.tensor_scalar(out=xn.ap[:], in0=xin.ap[:], scalar1=r.ap, scalar2=1.0,
                                              op0=ALU.mult, op1=ALU.mult), reads=[xin, r], writes=[xn])
        return xn

    def pre_b(xn, hT, col0, j, r, bank):
        bb = bank_bf(bank)

        def f(e):
            for kc in range(8):
                i = e.transpose(out=bb[:, kc * 128:(kc + 1) * 128], in_=xn.ap[:, kc * 128:(kc + 1) * 128],
                                identity=ident.ap[:])
            return i
        kb.op(pe, f, reads=[xn, ident], writes=[PB[bank]])
        for kc in range(8):
            kb.op(act, lambda e, kc=kc: e.activation(
                out=hT.ap[:, kc, col0:col0 + 128], in_=bb[:, kc * 128:(kc + 1) * 128], func=AF.Identity,
                scale=preA.ap[:, j, r, kc:kc + 1], bias=preB.ap[:, j, r, kc:kc + 1]),
                reads=[PB[bank], preA, preB], writes=[hT])

    try:
        with contextlib.ExitStack() as ph:
            cT = Buf(sb("cT", [128, 24], F32, ph))
            sT = Buf(sb("sT", [128, 24], BF16, ph))
            sbc = Buf(sb("sbc", [128, 3, 8, 128], BF16, ph))
            badaT = Buf(sb("badaT", [128, 72], F32, ph))
            gpreT = Buf(sb("gpreT", [128, 24], F32, ph))
            modT = Buf(sb("modT", [128, 9, 8, 3], F32, ph))
            wab = Buf(sb("wab", [128, 8, 1024], BF16, ph))
            stg = Ring([Buf(sb("p0st%d" % i, [128, 1024], F32, ph)) for i in range(3)])
            bbc = Buf(sb("bbc", [128, 1024], F32, ph))
            gpb = Buf(sb("gpb", [128, 1024], F32, ph))
            gtmp = Ring([Buf(sb("gtmp%d" % i, [128, 512], F32, ph)) for i in range(2)])
            gout = Ring([Buf(sb("gout%d" % i, [128, 512], F32, ph)) for i in range(2)])

            kb.dma(cT.ap[:], cT_d[:, :], writes=[cT])
            kb.dma(badaT.ap[:], badaT_d[:, :], writes=[badaT])
            kb.dma(gpreT.ap[:], gpreT_d[:, :], writes=[gpreT])
            kb.op(act, lambda e: e.activation(out=sT.ap[:], in_=cT.ap[:], func=AF.Silu), reads=[cT], writes=[sT])
            sTv = sT.ap[:].rearrange("p (k r) -> p k r", r=3)
            for r in range(3):
                kb.op(dve, lambda e, r=r: e.tensor_copy(out=sbc.ap[:, r], in_=sTv[:, :, r:r + 1].to_broadcast([128, 8, 128])),
                      reads=[sT], writes=[sbc])
            gidx = 0
            for m in range(9):
                for kc in range(8):
                    load_w(wab.ap[:, kc, :], wada_d[kc * 128:(kc + 1) * 128, m * 1024:(m + 1) * 1024], 1024, stg, wab)
                bk = m % 2

                def f(e):
                    for dc in range(8):
                        for kc in range(8):
                            i = e.matmul(out=PB[bk].ap[:, dc * 3:(dc + 1) * 3], lhsT=wab.ap[:, kc, dc * 128:(dc + 1) * 128],
                                         rhs=sT.ap[:, kc * 3:(kc + 1) * 3], start=(kc == 0), stop=(kc == 7))
                    return i
                kb.op(pe, f, reads=[wab, sT], writes=[PB[bk]])
                kb.op(dve, lambda e, m=m, bk=bk: e.tensor_tensor(
                    out=modT.ap[:, m], in0=PB[bk].ap[:, 0:24].rearrange("p (d r) -> p d r", r=3),
                    in1=badaT.ap[:, m * 8:(m + 1) * 8].unsqueeze(2).to_broadcast([128, 8, 3]), op=ALU.add),
                    reads=[PB[bk], badaT], writes=[modT])
                if m % 3 == 2:
                    j = m // 3
                    wj = 1.0 if j == 1 else 0.5
                    kb.dma(bbc.ap[:], bada_d[0:1, m * 1024:(m + 1) * 1024].to_broadcast([128, 1024]), writes=[bbc])
                    kb.dma(gpb.ap[:], gpost_d[j:j + 1, :].to_broadcast([128, 1024]), writes=[gpb])
                    for r in range(3):
                        for half in range(2):
                            bk2 = 2 + (gidx % 4)
                            gidx += 1

                            def f2(e, r=r, half=half, bk2=bk2):
                                for kc in range(8):
                                    i = e.matmul(out=PB[bk2].ap[:, :], lhsT=sbc.ap[:, r, kc, :],
                                                 rhs=wab.ap[:, kc, half * 512:(half + 1) * 512],
                                                 start=(kc == 0), stop=(kc == 7))
                                return i
                            kb.op(pe, f2, reads=[sbc, wab], writes=[PB[bk2]])
                            t1 = gtmp.next()
                            kb.op(dve, lambda e, bk2=bk2, half=half, t1=t1: e.tensor_tensor(
                                out=t1.ap[:], in0=PB[bk2].ap[:, :], in1=bbc.ap[:, half * 512:(half + 1) * 512], op=ALU.add),
                                reads=[PB[bk2], bbc], writes=[t1])
                            g1 = gout.next()
                            kb.op(dve, lambda e, t1=t1, g1=g1, half=half, wj=wj: e.scalar_tensor_tensor(
                                out=g1.ap[:], in0=t1.ap[:], scalar=wj, in1=gpb.ap[:, half * 512:(half + 1) * 512],
                                op0=ALU.mult, op1=ALU.mult), reads=[t1, gpb], writes=[g1])
                            kb.dma(gbc_d[j * 3 + r:j * 3 + r + 1, half * 512:(half + 1) * 512], g1.ap[0:1, :], reads=[g1])
            gpv = gpreT.ap[:].rearrange("p (j k) -> p j k", k=8)
            for j in range(3):
                for r in range(3):
                    kb.op(dve, lambda e, j=j, r=r: e.scalar_tensor_tensor(
                        out=preA.ap[:, j, r, :], in0=modT.ap[:, 3 * j + 1, :, r], scalar=1.0, in1=gpv[:, j, :],
                        op0=ALU.add, op1=ALU.mult), reads=[modT, gpreT], writes=[preA])
                    kb.op(dve, lambda e, j=j, r=r: e.tensor_copy(out=preB.ap[:, j, r, :], in_=modT.ap[:, 3 * j, :, r]),
                          reads=[modT], writes=[preB])
            kb.barrier()

        if lvl == 0:
            raise _Stop()

        def ffn_phase(wi_d, wo_d, tiles, j, name):
            with contextlib.ExitStack() as ph:
                win = [Buf(None) for _ in range(8)]
                wint = sb(name + "win", [128, 8, 2 * DFF], BF16, ph)
                for kc in range(8):
                    win[kc].ap = wint[:, kc, :]
                woutt = sb(name + "wout", [128, 22, D], BF16, ph)
                wout = [Buf(woutt[:, f, :]) for f in range(22)]
                hT = [Buf(sb(name + "hT%d" % i, [128, 8, 256], BF16, ph)) for i in range(2)]
                aTt = sb(name + "aT", [128, 22, 256], BF16, ph)
                aT = [Buf(aTt[:, f, :]) for f in range(22)]
                xin_ring = Ring([Buf(sb(name + "xin%d" % i, [128, D], F32, ph)) for i in range(2)])
                xn_ring = Ring([Buf(sb(name + "xn%d" % i, [128, D], BF16, ph)) for i in range(2)])
                sg_ring = Ring([Buf(sb(name + "sg%d" % i, [128, 256], F32, ph)) for i in range(2)])
                scr = Ring([Buf(sb(name + "scr%d" % i, [128, D], F32, ph)) for i in range(4)])
                rs_needed = sorted(set(t[2] for t in tiles))
                gbt = sb(name + "gbc", [128, 3, D], F32, ph)
                gb = {r: Buf(gbt[:, r, :]) for r in rs_needed}
                for r in rs_needed:
                    kb.dma(gb[r].ap, gbc_d[j * 3 + r:j * 3 + r + 1, :].to_broadcast([128, D]), writes=[gb[r]])
                for kc in range(8):
                    load_w(win[kc].ap, wi_d[kc * 128:(kc + 1) * 128, :], 2 * DFF, scr, win[kc])
                for f in range(22):
                    load_w(wout[f].ap, wo_d[f * 128:(f + 1) * 128, :], D, scr, wout[f])

                groups = [tiles[i:i + 2] for i in range(0, len(tiles), 2)]
                ng = len(groups)
                xns = {}
                for i, t in enumerate(groups[0]):
                    xns[(0, i)] = pre_a(t[0], xin_ring, xn_ring)
                for i, t in enumerate(groups[0]):
                    pre_b(xns[(0, i)], hT[0], i * 128, j, t[2], i)
                ycnt = 0
                for g in range(ng):
                    tl = groups[g]
                    n = 128 * len(tl)
                    h = hT[g % 2]
                    if g + 1 < ng:
                        for i, t in enumerate(groups[g + 1]):
                            xns[(g + 1, i)] = pre_a(t[0], xin_ring, xn_ring)
                    for f in range(22):
                        gbk, ubk = (f % 2) * 2, (f % 2) * 2 + 1

                        def mm(e, f=f, gbk=gbk, c0=0):
                            for kc in range(8):
                                i = e.matmul(out=PB[gbk].ap[:, 0:n], lhsT=win[kc].ap[:, c0 + f * 128:c0 + (f + 1) * 128],
                                             rhs=h.ap[:, kc, 0:n], start=(kc == 0), stop=(kc == 7))
                            return i
                        kb.op(pe, mm, reads=win + [h], writes=[PB[gbk]])
                        kb.op(pe, lambda e, f=f, ubk=ubk: mm(e, f, ubk, DFF), reads=win + [h], writes=[PB[ubk]])
                        sgb = sg_ring.next()
                        kb.op(act, lambda e, gbk=gbk, sgb=sgb: e.activation(out=sgb.ap[:, 0:n], in_=PB[gbk].ap[:, 0:n],
                                                                             func=AF.Silu), reads=[PB[gbk]], writes=[sgb])
                        kb.op(dve, lambda e, f=f, ubk=ubk, sgb=sgb: e.tensor_tensor(
                            out=aT[f].ap[:, 0:n], in0=sgb.ap[:, 0:n], in1=PB[ubk].ap[:, 0:n], op=ALU.mult),
                            reads=[sgb, PB[ubk]], writes=[aT[f]])
                    for i, t in enumerate(tl):
                        yp = 2 + (ycnt % 2)
                        ycnt += 1
                        for half in range(2):
                            bk = 2 * yp + half

                            def mm2(e, half=half, bk=bk, i=i):
                                for f in range(22):
                                    ii = e.matmul(out=PB[bk].ap[:, :], lhsT=aT[f].ap[:, i * 128:(i + 1) * 128],
                                                  rhs=wout[f].ap[:, half * 512:(half + 1) * 512],
                                                  start=(f == 0), stop=(f == 21))
                                return ii
                            kb.op(pe, mm2, reads=aT + wout, writes=[PB[bk]])
                        if g + 1 < ng and i < len(groups[g + 1]):
                            pre_b(xns[(g + 1, i)], hT[(g + 1) % 2], i * 128, j, groups[g + 1][i][2], i)
                        ybufs = [PB[2 * yp], PB[2 * yp + 1]]
                        yap = pp[yp][:, :]
                        ss = stat_ring.next()
                        kb.op(act, lambda e, yap=yap, ss=ss: e.activation(out=junk[:, :], in_=yap, func=AF.Square,
                                                                            accum_out=ss.ap), reads=ybufs, writes=[ss])
                        r = rstd_from_ss(ss, D)
                        xres = scr.next()
                        kb.dma(xres.ap[:], t[0], writes=[xres])
                        tt = scr.next()
                        kb.op(dve, lambda e, yap=yap, r=r, tt=tt, t=t: e.scalar_tensor_tensor(
                            out=tt.ap[:], in0=yap, scalar=r.ap, in1=gb[t[2]].ap, op0=ALU.mult, op1=ALU.mult),
                            reads=ybufs + [r, gb[t[2]]], writes=[tt])
                        kb.op(pool, lambda e, xres=xres, tt=tt: e.tensor_tensor(out=xres.ap[:], in0=xres.ap[:], in1=tt.ap[:],
                                                                                op=ALU.add), reads=[tt, xres], writes=[xres])
                        kb.dma(t[1], xres.ap[:], reads=[xres])
                kb.barrier()

        tiles1 = []
        for b in range(2):
            for i in range(2):
                tiles1.append((ctx_d[b, i * 128:(i + 1) * 128, :], x1_d[b, i * 128:(i + 1) * 128, :], 2))
        for b in range(2):
            for i in range(16):
                tiles1.append((x_d[b, i * 128:(i + 1) * 128, :], x1_d[b, L + i * 128:L + (i + 1) * 128, :], b))
        ffn_phase(f1i_d, f1o_d, tiles1, 0, "f1")
        if lvl == 1:
            raise _Stop()

        with contextlib.ExitStack() as ph:
            wmt = sb("wmix", [128, 8, 5376], BF16, ph)
            wm = [Buf(wmt[:, kc, :]) for kc in range(8)]
            hT = [Buf(sb("p2hT%d" % i, [128, 8, 256], BF16, ph)) for i in range(2)]
            xin_ring = Ring([Buf(sb("p2xin%d" % i, [128, D], F32, ph)) for i in range(2)])
            xn_ring = Ring([Buf(sb("p2xn%d" % i, [128, D], BF16, ph)) for i in range(2)])
            scr = Ring([Buf(sb("p2scr%d" % i, [128, D], F32, ph)) for i in range(2)])
            cosb = Ring([Buf(sb("cos%d" % i, [128, 640], F32, ph)) for i in range(1)])
            sinb = Ring([Buf(sb("sin%d" % i, [128, 640], F32, ph)) for i in range(1)])
            t1b = Buf(sb("ropet1", [128, 640], F32, ph))
            t2b = Buf(sb("ropet2", [128, 640], F32, ph))
            qkr = Ring([Buf(sb("qkr%d" % i, [128, 640], BF16, ph)) for i in range(2)])
            qkT = Ring([Buf(sb("qkT%d" % i, [128, 5, 128], BF16, ph)) for i in range(2)])
            vab = Ring([Buf(sb("vab%d" % i, [128, 128], BF16, ph)) for i in range(2)])
            f512 = Ring([Buf(sb("f512_%d" % i, [128, 512], F32, ph)) for i in range(12)])
            qh = Ring([Buf(sb("qh%d" % i, [128, 1024], BF16, ph)) for i in range(2)])
            kh = Ring([Buf(sb("kh%d" % i, [128, 1024], BF16, ph)) for i in range(2)])
            vhb = Ring([Buf(sb("vhb%d" % i, [128, 512], BF16, ph)) for i in range(2)])
            qkhT = Ring([Buf(sb("qkhT%d" % i, [128, 16, 128], BF16, ph)) for i in range(2)])
            sgst = Ring([Buf(sb("sgst%d" % i, [128, 4, 256], F32, ph)) for i in range(2)])
            sgtmp = Ring([Buf(sb("sgtmp%d" % i, [128, 256], F32, ph)) for i in range(2)])
            lmat = Buf(sb("lmat", [128, 256], F32, ph))
            cind = Buf(sb("cind", [128, 2], F32, ph))
            c0c1 = Buf(sb("c0c1", [128, 2, 2, 512], F32, ph))
            lbt = Buf(sb("lbt", [128, 2, 512], F32, ph))
            ghnb = Buf(sb("ghnb", [128, 512], F32, ph))

            kb.dma(lmat.ap[:], lmat_d[:, :], writes=[lmat])
            kb.dma(cind.ap[:], cind_d[:, :], writes=[cind])
            kb.dma(ghnb.ap[:], ghn_d[0:1, :].to_broadcast([128, 512]), writes=[ghnb])
            for di, lb_d in enumerate((lbf_d, lbb_d)):
                kb.dma(lbt.ap[:, 0, :], lb_d[0:1, :].to_broadcast([128, 512]), writes=[lbt])
                kb.dma(lbt.ap[:, 1, :], lb_d[1:2, :].to_broadcast([128, 512]), writes=[lbt])
                dd = f512.next()
                kb.op(dve, lambda e, dd=dd: e.tensor_tensor(out=dd.ap[:], in0=lbt.ap[:, 0, :], in1=lbt.ap[:, 1, :],
                                                           op=ALU.subtract), reads=[lbt], writes=[dd])
                th = f512.next()
                kb.op(act, lambda e, dd=dd, th=th: e.activation(out=th.ap[:], in_=dd.ap[:], func=AF.Tanh, scale=0.5),
                      reads=[dd], writes=[th])
                kb.op(dve, lambda e, th=th, di=di: e.tensor_scalar(out=c0c1.ap[:, di, 0, :], in0=th.ap[:], scalar1=0.25,
                                                                   scalar2=0.75, op0=ALU.mult, op1=ALU.add),
                      reads=[th], writes=[c0c1])
                kb.op(dve, lambda e, th=th, di=di: e.tensor_scalar(out=c0c1.ap[:, di, 1, :], in0=th.ap[:], scalar1=-0.25,
                                                                   scalar2=0.25, op0=ALU.mult, op1=ALU.add),
                      reads=[th], writes=[c0c1])
            for kc in range(8):
                load_w(wm[kc].ap, wmix_d[kc * 128:(kc + 1) * 128, :], 5376, scr, wm[kc])

            sbank = Ring([2, 3, 4, 5])

            def tm(bk, c0, n, h, i):
                def f(e):
                    for kc in range(8):
                        ii = e.matmul(out=PB[bk].ap[:, 0:n], lhsT=h.ap[:, kc, i * 128:(i + 1) * 128],
                                      rhs=wm[kc].ap[:, c0:c0 + n], start=(kc == 0), stop=(kc == 7))
                    return ii
                kb.op(pe, f, reads=wm + [h], writes=[PB[bk]])

            groups = []
            for b in range(2):
                groups.append([(b, 0), (b, 1)])
                for g in range(8):
                    groups.append([(b, 2 + 2 * g), (b, 3 + 2 * g)])
            ng = len(groups)

            def src_of(b, tt):
                return x1_d[b, tt * 128:(tt + 1) * 128, :]

            def rr(b, tt):
                return 2 if tt < 2 else b

            xns = {}
            for i, (b, tt) in enumerate(groups[0]):
                xns[(0, i)] = pre_a(src_of(b, tt), xin_ring, xn_ring)
            for i, (b, tt) in enumerate(groups[0]):
                pre_b(xns[(0, i)], hT[0], i * 128, 1, rr(b, tt), 6 + i)
            for g in range(ng):
                h = hT[g % 2]
                if g + 1 < ng:
                    for i, (b, tt) in enumerate(groups[g + 1]):
                        xns[(g + 1, i)] = pre_a(src_of(b, tt), xin_ring, xn_ring)
                for i, (b, tt) in enumerate(groups[g]):
                    lat = tt >= 2
                    lt = tt - 2
                    tok0 = tt * 128
                    if lat:
                        tm(0, 0, 512, h, i)
                        tm(1, 512, 128, h, i)
                        cb, sn = cosb.next(), sinb.next()
                        kb.dma(cb.ap[:], cos_d[lt * 128:(lt + 1) * 128, :], writes=[cb])
                        kb.dma(sn.ap[:], sin_d[lt * 128:(lt + 1) * 128, :], writes=[sn])
                        qkp = pp[0][:, 0:640]
                        kb.op(dve, lambda e, cb=cb: e.tensor_tensor(out=t1b.ap[:], in0=qkp, in1=cb.ap[:], op=ALU.mult),
                              reads=[PB[0], PB[1], cb], writes=[t1b])
                        qv = qkp.rearrange("p (s h e) -> p s h e", h=2, e=16)
                        sv = sn.ap[:].rearrange("p (s h e) -> p s h e", h=2, e=16)
                        tv = t2b.ap[:].rearrange("p (s h e) -> p s h e", h=2, e=16)
                        kb.op(dve, lambda e: e.tensor_tensor(out=tv[:, :, 0, :], in0=qv[:, :, 1, :], in1=sv[:, :, 0, :],
                                                             op=ALU.mult), reads=[PB[0], PB[1], sn], writes=[t2b])
                        kb.op(dve, lambda e: e.tensor_tensor(out=tv[:, :, 1, :], in0=qv[:, :, 0, :], in1=sv[:, :, 1, :],
                                                             op=ALU.mult), reads=[PB[0], PB[1], sn], writes=[t2b])
                        qk = qkr.next()
                        kb.op(pool, lambda e, qk=qk: e.tensor_tensor(
                            out=qk.ap[:, 0:512].rearrange("p (pr hh e) -> p hh pr e", pr=4, hh=2),
                            in0=t1b.ap[:, 0:512].rearrange("p (hh pr e) -> p hh pr e", hh=2, pr=4),
                            in1=t2b.ap[:, 0:512].rearrange("p (hh pr e) -> p hh pr e", hh=2, pr=4), op=ALU.add),
                            reads=[t1b, t2b], writes=[qk])
                        kb.op(pool, lambda e, qk=qk: e.tensor_tensor(out=qk.ap[:, 512:640], in0=t1b.ap[:, 512:640],
                                                                     in1=t2b.ap[:, 512:640], op=ALU.add),
                              reads=[t1b, t2b], writes=[qk])
                        b6 = bank_bf(6)

                        def ftr(e, qk=qk):
                            for c in range(5):
                                ii = e.transpose(out=b6[:, c * 128:(c + 1) * 128], in_=qk.ap[:, c * 128:(c + 1) * 128],
                                                 identity=ident.ap[:])
                            return ii
                        kb.op(pe, ftr, reads=[qk, ident], writes=[PB[6]])
                        qt = qkT.next()
                        kb.op(dve, lambda e, qt=qt: e.tensor_copy(out=qt.ap[:].rearrange("p c t -> p (c t)"),
                                                                  in_=b6[:, 0:640]), reads=[PB[6]], writes=[qt])
                        kb.dma(QT_d[b].rearrange("pr p t -> p pr t")[:, :, lt * 128:(lt + 1) * 128], qt.ap[:, 0:4, :],
                               reads=[qt])
                        kb.dma(KT_d[b, :, tok0:tok0 + 128], qt.ap[:, 4, :], reads=[qt])
                    else:
                        tm(0, 512, 128, h, i)
                        qk = qkr.next()
                        kb.op(dve, lambda e, qk=qk: e.tensor_copy(out=qk.ap[:, 0:128], in_=PB[0].ap[:, 0:128]),
                              reads=[PB[0]], writes=[qk])
                        b6 = bank_bf(6)
                        kb.op(pe, lambda e, qk=qk: e.transpose(out=b6[:, 0:128], in_=qk.ap[:, 0:128], identity=ident.ap[:]),
                              reads=[qk, ident], writes=[PB[6]])
                        qt = qkT.next()
                        kb.op(dve, lambda e, qt=qt: e.tensor_copy(out=qt.ap[:, 0, :], in_=b6[:, 0:128]),
                              reads=[PB[6]], writes=[qt])
                        kb.dma(KT_d[b, :, tok0:tok0 + 128], qt.ap[:, 0, :], reads=[qt])
                    bk = sbank.next()
                    tm(bk, 640, 128, h, i)
                    va = vab.next()
                    kb.op(dve, lambda e, va=va, bk=bk: e.tensor_copy(out=va.ap[:], in_=PB[bk].ap[:, 0:128]),
                          reads=[PB[bk]], writes=[va])
                    kb.dma(VA_d[b, tok0:tok0 + 128, :], va.ap[:], reads=[va])
                    bq = sbank.next()
                    tm(bq, 768, 512, h, i)
                    bf_ = sbank.next()
                    tm(bf_, 1280, 512, h, i)
                    bb_ = sbank.next()
                    tm(bb_, 1792, 512, h, i)
                    qs = f512.next()
                    kb.op(act, lambda e, qs=qs, bq=bq: e.activation(out=qs.ap[:], in_=PB[bq].ap[:, :], func=AF.Silu),
                          reads=[PB[bq]], writes=[qs])
                    ths = []
                    for bkx in (bf_, bb_):
                        th = f512.next()
                        kb.op(act, lambda e, th=th, bkx=bkx: e.activation(out=th.ap[:], in_=PB[bkx].ap[:, :], func=AF.Tanh,
                                                                           scale=0.5), reads=[PB[bkx]], writes=[th])
                        ths.append(th)
                    if lat:
                        bg = sbank.next()
                        tm(bg, 2816, 512, h, i)
                        sgt = f512.next()
                        kb.op(act, lambda e, sgt=sgt, bg=bg: e.activation(out=sgt.ap[:], in_=PB[bg].ap[:, :], func=AF.Silu),
                              reads=[PB[bg]], writes=[sgt])
                        g2 = f512.next()
                        kb.op(pool, lambda e, sgt=sgt, g2=g2: e.tensor_tensor(out=g2.ap[:], in0=sgt.ap[:], in1=ghnb.ap[:],
                                                                             op=ALU.mult), reads=[sgt, ghnb], writes=[g2])
                        kb.dma(G2_d[b, lt * 128:(lt + 1) * 128, :], g2.ap[:], reads=[g2])
                    bv = sbank.next()
                    tm(bv, 2304, 512, h, i)
                    vh = vhb.next()
                    kb.op(dve, lambda e, vh=vh, bv=bv: e.tensor_copy(out=vh.ap[:], in_=PB[bv].ap[:, :]),
                          reads=[PB[bv]], writes=[vh])
                    kb.dma(VH_d[b, tok0:tok0 + 128, :], vh.ap[:], reads=[vh])
                    fs, ks, lfs = [], [], []
                    for di in range(2):
                        th = ths[di]
                        ff = f512.next()
                        kb.op(pool, lambda e, th=th, ff=ff, di=di: e.tensor_tensor(out=ff.ap[:], in0=th.ap[:],
                                                                                  in1=c0c1.ap[:, di, 1, :], op=ALU.mult),
                              reads=[th, c0c1], writes=[ff])
                        kb.op(pool, lambda e, ff=ff, di=di: e.tensor_tensor(out=ff.ap[:], in0=ff.ap[:],
                                                                           in1=c0c1.ap[:, di, 0, :], op=ALU.add),
                              reads=[ff, c0c1], writes=[ff])
                        fs.append(ff)
                    for di in range(2):
                        ff = fs[di]
                        kk = ths[di]
                        kb.op(dve, lambda e, ff=ff, kk=kk: e.tensor_scalar(out=kk.ap[:], in0=ff.ap[:], scalar1=-1.0,
                                                                          scalar2=1.0, op0=ALU.mult, op1=ALU.add),
                              reads=[ff], writes=[kk])
                        ks.append(kk)
                        kb.op(act, lambda e, ff=ff: e.activation(out=ff.ap[:], in_=ff.ap[:], func=AF.Ln),
                              reads=[ff], writes=[ff])
                        lfs.append(ff)
                    qhb, khb = qh.next(), kh.next()
                    for di in range(2):
                        lf = lfs[di]
                        ba = sbank.next()
                        kb.op(pe, lambda e, lf=lf, ba=ba, di=di: e.matmul(out=PB[ba].ap[:, :],
                                                                          lhsT=lmat.ap[:, di * 128:(di + 1) * 128],
                                                                          rhs=lf.ap[:], start=True, stop=True),
                              reads=[lmat, lf], writes=[PB[ba]])
                        be = sbank.next()

                        def fet(e, lf=lf, be=be):
                            for hd in range(4):
                                ii = e.matmul(out=PB[be].ap[:, hd * 2:(hd + 1) * 2], lhsT=lf.ap[:, hd * 128:(hd + 1) * 128],
                                              rhs=cind.ap[:], start=True, stop=True)
                            return ii
                        kb.op(pe, fet, reads=[lf, cind], writes=[PB[be]])
                        ep, em = f512.next(), f512.next()
                        kb.op(act, lambda e, ep=ep, ba=ba: e.activation(out=ep.ap[:], in_=PB[ba].ap[:, :], func=AF.Exp),
                              reads=[PB[ba]], writes=[ep])
                        kb.op(act, lambda e, em=em, ba=ba: e.activation(out=em.ap[:], in_=PB[ba].ap[:, :], func=AF.Exp,
                                                                         scale=-1.0), reads=[PB[ba]], writes=[em])
                        kb.op(act, lambda e, be=be, di=di, b=b, tt=tt: e.activation(
                            out=elast.ap[:, b, di, :, 2 * tt:2 * tt + 2],
                            in_=PB[be].ap[:, 0:8].rearrange("p (h c) -> p h c", c=2), func=AF.Exp),
                            reads=[PB[be]], writes=[elast])
                        kb.op(dve, lambda e, ep=ep, di=di, qhb=qhb: e.tensor_tensor(
                            out=qhb.ap[:, di * 512:(di + 1) * 512], in0=qs.ap[:], in1=ep.ap[:], op=ALU.mult),
                            reads=[qs, ep], writes=[qhb])
                        kb.op(pool, lambda e, em=em, di=di, khb=khb, kk=ks[di]: e.tensor_tensor(
                            out=khb.ap[:, di * 512:(di + 1) * 512], in0=kk.ap[:], in1=em.ap[:], op=ALU.mult),
                            reads=[ks[di], em], writes=[khb])
                    kb.dma(KH_d[b, tok0:tok0 + 128, :], khb.ap[:], reads=[khb])
                    if lat:
                        b6, b7 = bank_bf(6), bank_bf(7)

                        def ftr2(e, src, dst):
                            for c in range(8):
                                ii = e.transpose(out=dst[:, c * 128:(c + 1) * 128], in_=src.ap[:, c * 128:(c + 1) * 128],
                                                 identity=ident.ap[:])
                            return ii
                        kb.op(pe, lambda e: ftr2(e, qhb, b6), reads=[qhb, ident], writes=[PB[6]])
                        kb.op(pe, lambda e: ftr2(e, khb, b7), reads=[khb, ident], writes=[PB[7]])
                        qk2 = qkhT.next()
                        kb.op(act, lambda e, qk2=qk2: e.copy(out=qk2.ap[:, 0:8, :].rearrange("p c t -> p (c t)"), in_=b6[:, :]),
                              reads=[PB[6]], writes=[qk2])
                        kb.op(dve, lambda e, qk2=qk2: e.tensor_copy(out=qk2.ap[:, 8:16, :].rearrange("p c t -> p (c t)"),
                                                                    in_=b7[:, :]), reads=[PB[7]], writes=[qk2])
                        kb.dma(QHT_d[b].rearrange("c p t -> p c t")[:, :, lt * 128:(lt + 1) * 128], qk2.ap[:, 0:8, :],
                               reads=[qk2])
                        kb.dma(KHT_d[b].rearrange("c p t -> p c t")[:, :, lt * 128:(lt + 1) * 128], qk2.ap[:, 8:16, :],
                               reads=[qk2])
                b, tt0 = groups[g][0]
                if tt0 >= 2:
                    ltok = (tt0 - 2) * 128
                    st = None
                    for cc in range(16):
                        bk = sbank.next()

                        def fg(e, cc=cc, bk=bk):
                            for kc in range(8):
                                ii = e.matmul(out=PB[bk].ap[:, 0:256],
                                              lhsT=wm[kc].ap[:, 3328 + cc * 128:3328 + (cc + 1) * 128],
                                              rhs=h.ap[:, kc, :], start=(kc == 0), stop=(kc == 7))
                            return ii
                        kb.op(pe, fg, reads=wm + [h], writes=[PB[bk]])
                        tmp = sgtmp.next()
                        kb.op(act, lambda e, tmp=tmp, bk=bk: e.activation(out=tmp.ap[:], in_=PB[bk].ap[:, 0:256], func=AF.Tanh,
                                                                           scale=0.5), reads=[PB[bk]], writes=[tmp])
                        if cc % 4 == 0:
                            st = sgst.next()
                        kb.op(dve, lambda e, tmp=tmp, st=st, cc=cc: e.tensor_scalar(out=st.ap[:, cc % 4, :], in0=tmp.ap[:],
                                                                                   scalar1=0.5, scalar2=0.5, op0=ALU.mult,
                                                                                   op1=ALU.add), reads=[tmp], writes=[st])
                        if cc % 4 == 3:
                            kb.dma(SG_d[b, cc - 3:cc + 1].rearrange("c p t -> p c t")[:, :, ltok:ltok + 256], st.ap[:],
                                   reads=[st])
                if g + 1 < ng:
                    for i, (b2, tt2) in enumerate(groups[g + 1]):
                        pre_b(xns[(g + 1, i)], hT[(g + 1) % 2], i * 128, 1, rr(b2, tt2), 6 + i)
            kb.barrier()

        if lvl == 2:
            raise _Stop()

        with contextlib.ExitStack() as ph:
            woat = sb("woa", [128, 4, D], BF16, ph)
            woa = Buf(woat)
            woh = Buf(sb("woh", [128, 4, D], BF16, ph))
            wo = Buf(sb("wo", [128, 8, D], BF16, ph))
            amask = Buf(sb("amask", [128, 3, 128], F32, ph))
            hmask = Buf(sb("hmask", [128, 128], F32, ph))
            sinkb = Buf(sb("sinkb", [128, 8], F32, ph))
            gb1t = sb("gb1", [128, 2, D], F32, ph)
            gb1 = [Buf(gb1t[:, r, :]) for r in range(2)]
            kb.dma(amask.ap[:].rearrange("p a b -> p (a b)"), amask_d[:, :], writes=[amask])
            kb.dma(hmask.ap[:], hmask_d[:, :], writes=[hmask])
            kb.dma(sinkb.ap[:], sink_d[0:1, :].to_broadcast([128, 8]), writes=[sinkb])
            for r in range(2):
                kb.dma(gb1[r].ap, gbc_d[3 + r:4 + r, :].to_broadcast([128, D]), writes=[gb1[r]])
            with contextlib.ExitStack() as pw:
                wst = Ring([Buf(sb("p3wst%d" % i, [128, D], F32, pw)) for i in range(3)])
                for c in range(4):
                    load_w(woa.ap[:, c, :], woa_d[c * 128:(c + 1) * 128, :], D, wst, woa)
                    load_w(woh.ap[:, c, :], woh_d[c * 128:(c + 1) * 128, :], D, wst, woh)
                for c in range(8):
                    load_w(wo.ap[:, c, :], wout_d[c * 128:(c + 1) * 128, :], D, wst, wo)
                kb.barrier()

            for b in range(2):
                with contextlib.ExitStack() as sq:
                    oaT = Buf(sb("oaT", [128, 4, T], BF16, sq))
                    with contextlib.ExitStack() as pa:
                        kts = Buf(sb("kts", [128, NTOK + 128], BF16, pa))
                        vts = Buf(sb("vts", [128, 18, 128], BF16, pa))
                        qts = Buf(sb("qts", [128, 4, T], BF16, pa))
                        pbuf = Ring([Buf(sb("pbuf%d" % i, [128, 640], BF16, pa)) for i in range(2)])
                        ptsb = Ring([Buf(sb("ptsb%d" % i, [128, 5, 128], BF16, pa)) for i in range(2)])
                        oat = Ring([Buf(sb("oat%d" % i, [128, 512], BF16, pa)) for i in range(2)])
                        kb.op(dve, lambda e: e.memset(kts.ap[:, NTOK:NTOK + 128], 0.0), writes=[kts])
                        kb.dma(kts.ap[:, 0:NTOK], KT_d[b, :, :], writes=[kts])
                        for t6 in range(0, 18, 6):
                            kb.dma(vts.ap[:, t6:t6 + 6, :], VA_d[b, t6 * 128:(t6 + 6) * 128, :].rearrange("(t p) e -> p t e", p=128),
                                   writes=[vts])
                        kb.dma(qts.ap[:], QT_d[b].rearrange("pr p t -> p pr t"), writes=[qts])
                        ucnt = 0
                        for n in range(16):
                            oa = oat.next()
                            for hh in range(2):
                                for pr in range(4):
                                    hd = hh * 4 + pr
                                    sp_ = ucnt % 2
                                    ptb = 4 + ucnt % 2
                                    ob = 6 + ucnt % 2
                                    ucnt += 1
                                    S = pp[sp_]
                                    sbufs = [PB[2 * sp_], PB[2 * sp_ + 1]]
                                    ps_ = slice(hh * 64, (hh + 1) * 64)

                                    def fs_(e, S=S, ps_=ps_, pr=pr, n=n):
                                        e.matmul(out=S[:, 128:512], lhsT=qts.ap[ps_, pr, n * 128:(n + 1) * 128],
                                                 rhs=kts.ap[ps_, 128 + n * 128:128 + n * 128 + 384], start=True, stop=True)
                                        return e.matmul(out=S[:, 512:768], lhsT=qts.ap[ps_, pr, n * 128:(n + 1) * 128],
                                                        rhs=kts.ap[ps_, 0:256], start=True, stop=True)
                                    kb.op(pe, fs_, reads=[qts, kts], writes=sbufs)
                                    mp = 2 if n == 0 else 0
                                    mn = 2 if n == 15 else 1
                                    kb.op(dve, lambda e, S=S, mp=mp: e.tensor_tensor(out=S[:, 128:256], in0=S[:, 128:256],
                                                                                     in1=amask.ap[:, mp, :], op=ALU.add),
                                          reads=[amask] + sbufs, writes=sbufs)
                                    kb.op(dve, lambda e, S=S, mn=mn: e.tensor_tensor(out=S[:, 384:512], in0=S[:, 384:512],
                                                                                     in1=amask.ap[:, mn, :], op=ALU.add),
                                          reads=[amask] + sbufs, writes=sbufs)
                                    mx = stat_ring.next()
                                    kb.op(dve, lambda e, S=S, mx=mx: e.reduce_max(out=mx.ap, in_=S[:, 128:768], axis=AX.X),
                                          reads=sbufs, writes=[mx])
                                    nb_ = stat_ring.next()
                                    kb.op(dve, lambda e, mx=mx, nb_=nb_, hd=hd: e.tensor_scalar(
                                        out=nb_.ap, in0=mx.ap, scalar1=0.125, scalar2=sinkb.ap[:, hd:hd + 1],
                                        op0=ALU.mult, op1=ALU.max), reads=[mx, sinkb], writes=[nb_])
                                    ngb = stat_ring.next()
                                    kb.op(dve, lambda e, nb_=nb_, ngb=ngb: e.tensor_scalar(
                                        out=ngb.ap, in0=nb_.ap, scalar1=-1.0, scalar2=None, op0=ALU.mult),
                                        reads=[nb_], writes=[ngb])
                                    P = pbuf.next()
                                    rs = stat_ring.next()
                                    kb.op(act, lambda e, S=S, P=P, ngb=ngb, rs=rs: e.activation(
                                        out=P.ap[:], in_=S[:, 128:768], func=AF.Exp, bias=ngb.ap, scale=0.125,
                                        accum_out=rs.ap), reads=sbufs + [ngb], writes=[P, rs])
                                    es_ = stat_ring.next()
                                    kb.op(act, lambda e, es_=es_, ngb=ngb, hd=hd: e.activation(
                                        out=es_.ap, in_=sinkb.ap[:, hd:hd + 1], func=AF.Exp, bias=ngb.ap, scale=1.0),
                                        reads=[sinkb, ngb], writes=[es_])
                                    den = stat_ring.next()
                                    kb.op(dve, lambda e, den=den, rs=rs, es_=es_: e.tensor_tensor(out=den.ap, in0=rs.ap,
                                                                                                in1=es_.ap, op=ALU.add),
                                          reads=[rs, es_], writes=[den])
                                    rden = stat_ring.next()
                                    kb.op(dve, lambda e, den=den, rden=rden: e.reciprocal(out=rden.ap, in_=den.ap),
                                          reads=[den], writes=[rden])
                                    js = [j_ for j_ in range(5) if not (n == 0 and j_ == 0) and not (n == 15 and j_ == 2)]
                                    ptb_bf = bank_bf(ptb)

                                    def ftp(e, P=P, js=js, ptb_bf=ptb_bf):
                                        for j_ in js:
                                            ii = e.transpose(out=ptb_bf[:, j_ * 128:(j_ + 1) * 128],
                                                             in_=P.ap[:, j_ * 128:(j_ + 1) * 128], identity=ident.ap[:])
                                        return ii
                                    kb.op(pe, ftp, reads=[P, ident], writes=[PB[ptb]])
                                    pt = ptsb.next()
                                    lo, hi = js[0], js[-1] + 1
                                    if ucnt % 2 == 0:
                                        kb.op(dve, lambda e, pt=pt, lo=lo, hi=hi, ptb_bf=ptb_bf: e.tensor_copy(
                                            out=pt.ap[:, lo:hi, :].rearrange("p c t -> p (c t)"),
                                            in_=ptb_bf[:, lo * 128:hi * 128]), reads=[PB[ptb]], writes=[pt])
                                    else:
                                        kb.op(act, lambda e, pt=pt, lo=lo, hi=hi, ptb_bf=ptb_bf: e.copy(
                                            out=pt.ap[:, lo:hi, :].rearrange("p c t -> p (c t)"),
                                            in_=ptb_bf[:, lo * 128:hi * 128]), reads=[PB[ptb]], writes=[pt])
                                    vt = {0: 1 + n, 1: 2 + n, 2: 3 + n, 3: 0, 4: 1}

                                    def fpv(e, pt=pt, js=js, ob=ob, ps_=ps_, vt=vt):
                                        for q_, j_ in enumerate(js):
                                            ii = e.matmul(out=PB[ob].ap[:, 0:64], lhsT=pt.ap[:, j_, :],
                                                          rhs=vts.ap[:, vt[j_], ps_], start=(q_ == 0),
                                                          stop=(q_ == len(js) - 1))
                                        return ii
                                    kb.op(pe, fpv, reads=[pt, vts], writes=[PB[ob]])
                                    kb.op(act, lambda e, oa=oa, ob=ob, rden=rden, hd=hd: e.activation(
                                        out=oa.ap[:, hd * 64:(hd + 1) * 64], in_=PB[ob].ap[:, 0:64], func=AF.Identity,
                                        scale=rden.ap), reads=[PB[ob], rden], writes=[oa])
                            ptb = 4 + ucnt % 2
                            ptb_bf = bank_bf(ptb)

                            def fto(e, oa=oa, ptb_bf=ptb_bf):
                                for c in range(4):
                                    ii = e.transpose(out=ptb_bf[:, c * 128:(c + 1) * 128], in_=oa.ap[:, c * 128:(c + 1) * 128],
                                                     identity=ident.ap[:])
                                return ii
                            kb.op(pe, fto, reads=[oa, ident], writes=[PB[ptb]])
                            kb.op(dve, lambda e, n=n, ptb_bf=ptb_bf: e.tensor_copy(
                                out=oaT.ap[:, :, n * 128:(n + 1) * 128],
                                in_=ptb_bf[:, 0:512].rearrange("p (c t) -> p c t", c=4)), reads=[PB[ptb]], writes=[oaT])
                        if dbg:
                            kb.dma(OAT_d[b], oaT.ap[:], reads=[oaT])
                        kb.barrier()
                    if lvl == 2.2:
                        raise _Stop()
                    with contextlib.ExitStack() as pb_:
                        sdt = sb("sdec", [128, 8, 32, 128], BF16, pb_)
                        sdec = [Buf(sdt[:, c]) for c in range(8)]
                        vhs = Buf(sb("vhs", [128, 18, 512], BF16, pb_))
                        for t6 in range(0, 18, 6):
                            kb.dma(vhs.ap[:, t6:t6 + 6, :], VH_d[b, t6 * 128:(t6 + 6) * 128, :].rearrange("(t p) e -> p t e", p=128),
                                   writes=[vhs])
                        with contextlib.ExitStack() as pb2:
                            khs = Buf(sb("khs", [128, 18, 1024], BF16, pb2))
                            Stt = sb("Sst", [128, 8, 128], F32, pb2)
                            Sst = [Buf(Stt[:, c, :]) for c in range(8)]
                            for t6 in range(0, 18, 6):
                                kb.dma(khs.ap[:, t6:t6 + 6, :],
                                       KH_d[b, t6 * 128:(t6 + 6) * 128, :].rearrange("(t p) e -> p t e", p=128), writes=[khs])
                            kvb = PB
                            for c in range(8):
                                kb.op(dve, lambda e, c=c: e.memset(Sst[c].ap, 0.0), writes=[Sst[c]])
                            orders = [list(range(36)), [3, 2, 1, 0] + list(range(35, 3, -1))]
                            for step in range(36):
                                for di in range(2):
                                    for hd in range(4):
                                        c = di * 4 + hd
                                        ch = orders[di][step]
                                        tt, p = ch // 2, ch % 2
                                        ps_ = slice(p * 64, (p + 1) * 64)
                                        kb.op(pe, lambda e, c=c, tt=tt, ps_=ps_, hd=hd: e.matmul(
                                            out=kvb[c].ap[:, 0:128], lhsT=khs.ap[ps_, tt, c * 128:(c + 1) * 128],
                                            rhs=vhs.ap[ps_, tt, hd * 128:(hd + 1) * 128], start=True, stop=True),
                                            reads=[khs, vhs], writes=[kvb[c]])
                                        ev = elast.ap[:, b, di, hd, ch:ch + 1]
                                        if ch >= 4:
                                            kb.op(act, lambda e, c=c, ch=ch, ev=ev: e.activation(
                                                out=sdec[c].ap[:, ch - 4, :], in_=Sst[c].ap, func=AF.Identity, scale=ev),
                                                reads=[Sst[c], elast], writes=[sdec[c]])
                                        kb.op(dve, lambda e, c=c, ev=ev: e.scalar_tensor_tensor(
                                            out=Sst[c].ap, in0=Sst[c].ap, scalar=ev, in1=kvb[c].ap[:, 0:128], op0=ALU.mult,
                                            op1=ALU.add), reads=[Sst[c], kvb[c], elast], writes=[Sst[c]])
                            if dbg:
                                for c in range(8):
                                    kb.dma(SDEC_d[b, :, c], sdec[c].ap, reads=[sdec[c]])
                            kb.barrier()
                        if lvl == 2.3:
                            raise _Stop()
                        with contextlib.ExitStack() as pc:
                            qhr = Ring([Buf(sb("qhr%d" % i, [128, 8, 128], BF16, pc)) for i in range(2)])
                            khr = Ring([Buf(sb("khr%d" % i, [128, 8, 128], BF16, pc)) for i in range(2)])
                            g2r = Ring([Buf(sb("g2r%d" % i, [128, 512], F32, pc)) for i in range(1)])
                            scTr = Ring([Buf(sb("scT%d" % i, [128, 128], BF16, pc)) for i in range(2)])
                            orr = Ring([Buf(sb("orr%d" % i, [128, 512], BF16, pc)) for i in range(2)])
                            sgr = Ring([Buf(sb("sgr%d" % i, [128, 16, 128], F32, pc)) for i in range(1)])
                            scr = Ring([Buf(sb("p3scr%d" % i, [128, D], F32, pc)) for i in range(4)])
                            orTr = Ring([Buf(sb("orT%d" % i, [128, 4, 128], BF16, pc)) for i in range(2)])
                            mTr = Ring([Buf(sb("mT%d" % i, [128, 8, 128], BF16, pc)) for i in range(2)])
                            for lt in range(16):
                                tt = lt + 2
                                tsl = slice(lt * 128, (lt + 1) * 128)
                                qhT, khT, g2t = qhr.next(), khr.next(), g2r.next()
                                kb.dma(qhT.ap[:], QHT_d[b].rearrange("c p t -> p c t")[:, :, tsl], writes=[qhT])
                                kb.dma(khT.ap[:], KHT_d[b].rearrange("c p t -> p c t")[:, :, tsl], writes=[khT])
                                kb.dma(g2t.ap[:], G2_d[b, tsl, :], writes=[g2t])
                                sgt = sgr.next()
                                kb.dma(sgt.ap[:], SG_d[b].rearrange("c p t -> p c t")[:, :, tsl], writes=[sgt])
                                for hd in range(4):
                                    def fsc(e, hd=hd):
                                        for p in range(2):
                                            for di in range(2):
                                                ii = e.matmul(out=PB[6].ap[p * 64:(p + 1) * 64, di * 64:(di + 1) * 64],
                                                              lhsT=khT.ap[:, di * 4 + hd, p * 64:(p + 1) * 64],
                                                              rhs=qhT.ap[:, di * 4 + hd, p * 64:(p + 1) * 64],
                                                              start=True, stop=True)
                                        return ii
                                    kb.op(pe, fsc, reads=[khT, qhT], writes=[PB[6]])
                                    scT = scTr.next()
                                    kb.op(dve, lambda e, scT=scT: e.tensor_tensor(out=scT.ap[:], in0=PB[6].ap[:, 0:128],
                                                                                  in1=hmask.ap[:], op=ALU.mult),
                                          reads=[PB[6], hmask], writes=[scT])

                                    def fo(e, hd=hd, scT=scT):
                                        for p in range(2):
                                            ch = 2 * lt + p
                                            ps_ = slice(p * 64, (p + 1) * 64)
                                            o_ = PB[7].ap[ps_, hd * 128:(hd + 1) * 128]
                                            e.matmul(out=o_, lhsT=qhT.ap[:, hd, ps_], rhs=sdec[hd].ap[:, ch, :],
                                                     start=True, stop=False)
                                            e.matmul(out=o_, lhsT=scT.ap[ps_, 0:64], rhs=vhs.ap[ps_, tt, hd * 128:(hd + 1) * 128],
                                                     start=False, stop=False)
                                            e.matmul(out=o_, lhsT=qhT.ap[:, 4 + hd, ps_], rhs=sdec[4 + hd].ap[:, ch, :],
                                                     start=False, stop=False)
                                            ii = e.matmul(out=o_, lhsT=scT.ap[ps_, 64:128],
                                                          rhs=vhs.ap[ps_, tt, hd * 128:(hd + 1) * 128], start=False, stop=True)
                                        return ii
                                    kb.op(pe, fo, reads=[qhT, scT, vhs, sdec[hd], sdec[4 + hd]], writes=[PB[7]])
                                ss = stat_ring.next()
                                kb.op(act, lambda e, ss=ss: e.activation(out=junk[:, 0:512], in_=PB[7].ap[:, :], func=AF.Square,
                                                                          accum_out=ss.ap), reads=[PB[7]], writes=[ss])
                                r = rstd_from_ss(ss, 512)
                                orb = orr.next()
                                kb.op(dve, lambda e, r=r, orb=orb, g2t=g2t: e.scalar_tensor_tensor(
                                    out=orb.ap[:], in0=PB[7].ap[:, :], scalar=r.ap, in1=g2t.ap[:], op0=ALU.mult, op1=ALU.mult),
                                    reads=[PB[7], r, g2t], writes=[orb])
                                b6 = bank_bf(6)

                                def ftr3(e, orb=orb):
                                    for c in range(4):
                                        ii = e.transpose(out=b6[:, c * 128:(c + 1) * 128], in_=orb.ap[:, c * 128:(c + 1) * 128],
                                                         identity=ident.ap[:])
                                    return ii
                                kb.op(pe, ftr3, reads=[orb, ident], writes=[PB[6]])
                                orT = orTr.next()
                                kb.op(act, lambda e, orT=orT: e.copy(out=orT.ap[:].rearrange("p c t -> p (c t)"), in_=b6[:, 0:512]),
                                      reads=[PB[6]], writes=[orT])
                                for (w_, src, pr_, sl_) in ((woa, oaT, 0, tsl), (woh, orT, 1, slice(0, 128))):
                                    def fy(e, w_=w_, src=src, pr_=pr_, sl_=sl_):
                                        for dc in range(8):
                                            for c in range(4):
                                                ii = e.matmul(out=pp[pr_][:, dc * 128:(dc + 1) * 128],
                                                              lhsT=w_.ap[:, c, dc * 128:(dc + 1) * 128], rhs=src.ap[:, c, sl_],
                                                              start=(c == 0), stop=(c == 3))
                                        return ii
                                    kb.op(pe, fy, reads=[w_, src], writes=[PB[2 * pr_], PB[2 * pr_ + 1]])
                                t1, t2 = scr.next(), scr.next()
                                kb.op(dve, lambda e, t1=t1, sgt=sgt: e.tensor_tensor(
                                    out=t1.ap[:], in0=pp[0][:, :], in1=sgt.ap[:, 0:8, :].rearrange("p c t -> p (c t)"),
                                    op=ALU.mult), reads=[PB[0], PB[1], sgt], writes=[t1])
                                kb.op(dve, lambda e, t2=t2, sgt=sgt: e.tensor_tensor(
                                    out=t2.ap[:], in0=pp[1][:, :], in1=sgt.ap[:, 8:16, :].rearrange("p c t -> p (c t)"),
                                    op=ALU.mult), reads=[PB[2], PB[3], sgt], writes=[t2])
                                mT = mTr.next()
                                kb.op(pool, lambda e, t1=t1, t2=t2, mT=mT: e.tensor_tensor(
                                    out=mT.ap[:].rearrange("p c t -> p (c t)"), in0=t1.ap[:], in1=t2.ap[:], op=ALU.add),
                                    reads=[t1, t2], writes=[mT])
                                for half in range(2):
                                    def fy2(e, half=half, mT=mT):
                                        for dc in range(8):
                                            ii = e.matmul(out=PB[4 + half].ap[:, :], lhsT=mT.ap[:, dc, :],
                                                          rhs=wo.ap[:, dc, half * 512:(half + 1) * 512],
                                                          start=(dc == 0), stop=(dc == 7))
                                        return ii
                                    kb.op(pe, fy2, reads=[mT, wo], writes=[PB[4 + half]])
                                ss = stat_ring.next()
                                kb.op(act, lambda e, ss=ss: e.activation(out=junk[:, :], in_=pp[2][:, :], func=AF.Square,
                                                                          accum_out=ss.ap), reads=[PB[4], PB[5]], writes=[ss])
                                r = rstd_from_ss(ss, D)
                                xres, tq = scr.next(), scr.next()
                                kb.dma(xres.ap[:], x1_d[b, L + lt * 128:L + (lt + 1) * 128, :], writes=[xres])
                                kb.op(dve, lambda e, r=r, tq=tq: e.scalar_tensor_tensor(
                                    out=tq.ap[:], in0=pp[2][:, :], scalar=r.ap, in1=gb1[b].ap, op0=ALU.mult, op1=ALU.mult),
                                    reads=[PB[4], PB[5], r, gb1[b]], writes=[tq])
                                kb.op(pool, lambda e, xres=xres, tq=tq: e.tensor_tensor(out=xres.ap[:], in0=xres.ap[:],
                                                                                       in1=tq.ap[:], op=ALU.add),
                                      reads=[xres, tq], writes=[xres])
                                kb.dma(x2_d[b, tsl, :], xres.ap[:], reads=[xres])
                            kb.barrier()
                if lvl == 2.4:
                    raise _Stop()

        if lvl == 3:
            raise _Stop()

        tiles2 = []
        for b in range(2):
            for i in range(16):
                tiles2.append((x2_d[b, i * 128:(i + 1) * 128, :], out_d[b, i * 128:(i + 1) * 128, :], b))
        ffn_phase(f2i_d, f2o_d, tiles2, 2, "f2")
    except _Stop:
        kb.barrier()
        if lvl not in (0, 1, 2, 3):
            return nc
    es.close()
    return nc


def _consts():
    c = {}
    c["c_ident"] = np.eye(128, dtype=np.float32).astype(ml_dtypes.bfloat16)
    i = np.arange(128)[:, None]
    j = np.arange(128)[None, :]
    am = np.zeros((128, 3, 128), np.float32)
    am[:, 0, :] = np.where(j >= i, 0.0, NEG)
    am[:, 1, :] = np.where(j <= i, 0.0, NEG)
    am[:, 2, :] = NEG
    c["c_amask"] = am.reshape(128, 384)
    s = np.arange(128)[:, None] % 64
    t = np.arange(64)[None, :]
    hm = np.zeros((128, 128), np.float32)
    hm[:, 0:64] = (s <= t)
    hm[:, 64:128] = (s >= t)
    c["c_hmask"] = hm
    ss = np.arange(128)[:, None]
    tt = np.arange(128)[None, :]
    same = (ss // 64) == (tt // 64)
    lm = np.zeros((128, 256), np.float32)
    lm[:, 0:128] = np.where(same & (ss > tt), -1.0, 0.0)
    lm[:, 128:256] = np.where(same & (ss < tt), -1.0, 0.0)
    c["c_lmat"] = lm
    ci = np.zeros((128, 2), np.float32)
    ci[0:64, 0] = 1.0
    ci[64:128, 1] = 1.0
    c["c_cind"] = ci
    pos = np.arange(T)
    row = (pos // 64).astype(np.float32)
    col = (pos % 64).astype(np.float32)
    freqs = (np.float32(10000.0) ** (-np.arange(16, dtype=np.float32) / np.float32(16))).astype(np.float32)
    cosm = np.zeros((T, 64), np.float32)
    sinm = np.zeros((T, 64), np.float32)
    for seg, p_ in enumerate((row, col)):
        ang = (p_[:, None] * freqs[None, :]).astype(np.float32)
        cs, sn = np.cos(ang).astype(np.float32), np.sin(ang).astype(np.float32)
        cosm[:, seg * 32:seg * 32 + 16] = cs
        cosm[:, seg * 32 + 16:seg * 32 + 32] = cs
        sinm[:, seg * 32:seg * 32 + 16] = -sn
        sinm[:, seg * 32 + 16:seg * 32 + 32] = sn
    c["c_cos"] = np.ascontiguousarray(np.tile(cosm, (1, 10)))
    c["c_sin"] = np.ascontiguousarray(np.tile(sinm, (1, 10)))
    return c


def _in_maps(inp):
    f = lambda a: np.ascontiguousarray(np.asarray(a, dtype=np.float32))
    x, c, ctx, c_ctx = f(inp["x"]), f(inp["c"]), f(inp["ctx"]), f(inp["c_ctx"])
    consts = _consts()
    shared = {
        "w_ada": f(inp["w_ada"])[0], "b_ada": f(inp["b_ada"]),
        "b_adaT": np.ascontiguousarray(f(inp["b_ada"])[0].reshape(72, 128).T),
        "gpreT": np.ascontiguousarray(f(inp["norm_pre"])[0].reshape(3, 8, 128).transpose(2, 0, 1).reshape(128, 24)),
        "norm_post": f(inp["norm_post"])[0],
        "ffn1_w_in": f(inp["ffn1_w_in"])[0], "ffn1_w_out": f(inp["ffn1_w_out"])[0],
        "ffn2_w_in": f(inp["ffn2_w_in"])[0], "ffn2_w_out": f(inp["ffn2_w_out"])[0],
        "mix_w_in": f(inp["mix_w_in"])[0], "attn_sink": f(inp["attn_sink"]),
        "hgrn_lb_fwd": f(inp["hgrn_lb_fwd"]), "hgrn_lb_bwd": f(inp["hgrn_lb_bwd"]),
        "hgrn_norm": f(inp["hgrn_norm"]), "w_o_attn": f(inp["w_o_attn"])[0], "w_o_hgrn": f(inp["w_o_hgrn"])[0],
        "w_out": f(inp["w_out"])[0],
    }
    shared.update(consts)
    maps = []
    for k in range(NCORES):
        cv = np.stack([c[2 * k], c[2 * k + 1], c_ctx], 0)
        cT = np.ascontiguousarray(cv.reshape(3, 8, 128).transpose(2, 1, 0).reshape(128, 24))
        m = dict(shared)
        m["x"] = np.ascontiguousarray(x[2 * k:2 * k + 2])
        m["ctx"] = np.ascontiguousarray(ctx[2 * k:2 * k + 2])
        m["cT"] = cT
        maps.append(m)
    return maps


def kernel(**inputs):
    nc = build_program(False)
    maps = _in_maps(inputs)
    res = run_bass_kernel_spmd(nc, maps, core_ids=list(range(NCORES)))
    out = np.concatenate([np.asarray(r["out"], dtype=np.float32) for r in res.results], axis=0)
    return out
```
